# Optimizing a Trainium2 kernel written in Bass

```python
import jax, jax.numpy as jnp
from jax import lax
import numpy as np

D_MODEL = 1024
BATCH = 8
SEQ = 4096
DEPTH = 4
DEC_BATCH = 4
DEC_SEQ = 4096
PAST_LEN = 128

MIX_WIDTH = D_MODEL
ATTN_HEAD_DIM = 64
ATTN_WIDTH = MIX_WIDTH // 2
ATTN_HEADS = ATTN_WIDTH // ATTN_HEAD_DIM
ATTN_KV_HEADS = 2
ATTN_GROUP = ATTN_HEADS // ATTN_KV_HEADS
KV_WIDTH = ATTN_KV_HEADS * ATTN_HEAD_DIM
WINDOW = 128
BLOCK = 128
RET_WIDTH = MIX_WIDTH - ATTN_WIDTH
RET_HEAD_DIM = 128
RET_HEADS = RET_WIDTH // RET_HEAD_DIM
RET_CHUNK = 128
D_FF = 4 * D_MODEL
N_MOD = 6
EPS = 1e-6
NEG_INF = -1e30
IN_SIZES = (ATTN_WIDTH, KV_WIDTH, KV_WIDTH, RET_WIDTH, RET_WIDTH, RET_WIDTH, RET_WIDTH)
IN_WIDTH = sum(IN_SIZES)
IN_OFFSETS = tuple(int(o) for o in np.cumsum(IN_SIZES)[:-1])

kernel_name = "hymba_swa_retention_adaln_encoder"


def rms_norm(x, g):
    xf = x.astype(jnp.float32)
    y = xf * lax.rsqrt(jnp.mean(xf * xf, axis=-1, keepdims=True) + EPS)
    return (y * g.astype(jnp.float32)).astype(x.dtype)


def alibi_slopes():
    return jnp.exp2(-8.0 * (jnp.arange(ATTN_HEADS, dtype=jnp.float32) + 1.0) / ATTN_HEADS)


def windowed_attention(q, k, v, sink):
    B, S = q.shape[0], q.shape[1]
    N = S // BLOCK
    qb = q.astype(jnp.float32).reshape(B, N, BLOCK, ATTN_KV_HEADS, ATTN_GROUP, ATTN_HEAD_DIM)
    pad = ((0, 0), (BLOCK, BLOCK), (0, 0), (0, 0))
    kp = jnp.pad(k.astype(jnp.float32), pad).reshape(B, N + 2, BLOCK, ATTN_KV_HEADS, ATTN_HEAD_DIM)
    vp = jnp.pad(v.astype(jnp.float32), pad).reshape(B, N + 2, BLOCK, ATTN_KV_HEADS, ATTN_HEAD_DIM)
    kb = jnp.concatenate([kp[:, :-2], kp[:, 1:-1], kp[:, 2:]], axis=2)
    vb = jnp.concatenate([vp[:, :-2], vp[:, 1:-1], vp[:, 2:]], axis=2)
    s = jnp.einsum('bnikgd,bnjkd->bnkgij', qb, kb) * (ATTN_HEAD_DIM ** -0.5)
    i = jnp.arange(BLOCK)[:, None]
    j = jnp.arange(3 * BLOCK)[None, :]
    dist = jnp.abs(j - BLOCK - i)
    key_pos = jnp.arange(N)[:, None, None] * BLOCK - BLOCK + j[None]
    valid = (dist <= WINDOW)[None] & (key_pos >= 0) & (key_pos < S)
    slopes = alibi_slopes().reshape(ATTN_KV_HEADS, ATTN_GROUP)
    bias = -slopes[:, :, None, None] * dist.astype(jnp.float32)
    s = jnp.where(valid[None, :, None, None], s + bias, NEG_INF)
    sinkf = sink.astype(jnp.float32).reshape(ATTN_KV_HEADS, ATTN_GROUP)[:, :, None, None]
    m = jnp.maximum(jnp.max(s, axis=-1, keepdims=True), sinkf)
    p = jnp.exp(s - m)
    denom = jnp.sum(p, axis=-1, keepdims=True) + jnp.exp(sinkf - m)
    o = jnp.einsum('bnkgij,bnjkd->bnikgd', p / denom, vb)
    return o.reshape(B, S, ATTN_WIDTH)


def retention_direction(q, k, v, log_g, strict):
    B, S, H, d = q.shape
    C = RET_CHUNK
    N = S // C
    qc = q.reshape(B, N, C, H, d)
    kc = k.reshape(B, N, C, H, d) * (d ** -0.5)
    vc = v.reshape(B, N, C, H, d)
    pos = jnp.arange(C, dtype=jnp.float32)
    diff = pos[:, None] - pos[None, :]
    mask = (diff > 0) if strict else (diff >= 0)
    dmat = jnp.where(mask[None], jnp.exp(jnp.maximum(diff, 0.0)[None] * log_g[:, None, None]), 0.0)
    scores = jnp.einsum('bnihd,bnjhd->bnhij', qc, kc) * dmat
    inner = jnp.einsum('bnhij,bnjhe->bnihe', scores, vc)
    k_dec = kc * jnp.exp((C - 1.0 - pos)[:, None] * log_g[None, :])[:, :, None]
    kv = jnp.einsum('bnjhd,bnjhe->nbhde', k_dec, vc)
    chunk_decay = jnp.exp(C * log_g)[None, :, None, None]

    def step(r, kv_n):
        return r * chunk_decay + kv_n, r

    _, r_prev = lax.scan(step, jnp.zeros_like(kv[0]), kv)
    q_dec = qc * jnp.exp((pos + 1.0)[:, None] * log_g[None, :])[:, :, None]
    cross = jnp.einsum('bnihd,nbhde->bnihe', q_dec, r_prev)
    return (inner + cross).reshape(B, S, H, d)


def bidirectional_retention(q, k, v, g, decay_fwd, decay_bwd):
    B, S = q.shape[0], q.shape[1]
    shp = (B, S, RET_HEADS, RET_HEAD_DIM)
    qf = q.astype(jnp.float32).reshape(shp)
    kf = k.astype(jnp.float32).reshape(shp)
    vf = v.astype(jnp.float32).reshape(shp)
    lg_f = jax.nn.log_sigmoid(decay_fwd.astype(jnp.float32))
    lg_b = jax.nn.log_sigmoid(decay_bwd.astype(jnp.float32))
    y_f = retention_direction(qf, kf, vf, lg_f, False)
    y_b = jnp.flip(retention_direction(jnp.flip(qf, 1), jnp.flip(kf, 1), jnp.flip(vf, 1), lg_b, True), 1)
    y = y_f + y_b
    mu = jnp.mean(y, axis=-1, keepdims=True)
    yc = y - mu
    yn = yc * lax.rsqrt(jnp.mean(yc * yc, axis=-1, keepdims=True) + EPS)
    return jax.nn.silu(g.astype(jnp.float32)) * yn.reshape(B, S, RET_WIDTH)


def run_trunk(x, c, w_ada, b_ada, norm1_g, w_in, attn_sink, ret_decay_fwd, ret_decay_bwd,
              w_out, norm2_g, w_mlp1, w_mlp2, final_g):
    B, S, _ = x.shape
    c_act = jax.nn.silu(c)
    for l in range(DEPTH):
        mod = c_act @ w_ada[l] + b_ada[l]
        sh1, sc1, g1, sh2, sc2, g2 = [m[:, None, :] for m in jnp.split(mod, N_MOD, axis=-1)]
        h = rms_norm(x, norm1_g[l]) * (1.0 + sc1) + sh1
        proj = h @ w_in[l]
        q_a, k_a, v_a, q_r, k_r, v_r, g_r = jnp.split(proj, IN_OFFSETS, axis=-1)
        o_a = windowed_attention(q_a.reshape(B, S, ATTN_HEADS, ATTN_HEAD_DIM),
                                 k_a.reshape(B, S, ATTN_KV_HEADS, ATTN_HEAD_DIM),
                                 v_a.reshape(B, S, ATTN_KV_HEADS, ATTN_HEAD_DIM),
                                 attn_sink[l])
        o_r = bidirectional_retention(q_r, k_r, v_r, g_r, ret_decay_fwd[l], ret_decay_bwd[l])
        mix = jnp.concatenate([o_a, o_r], axis=-1).astype(x.dtype)
        x = x + g1 * (mix @ w_out[l])
        h = rms_norm(x, norm2_g[l]) * (1.0 + sc2) + sh2
        x = x + g2 * (jnp.square(jax.nn.relu(h @ w_mlp1[l])) @ w_mlp2[l])
    return rms_norm(x, final_g)


def setup_inputs(seed: int = 0) -> dict:
    key = jax.random.key(seed)
    ks = jax.random.split(key, 16)
    f32 = jnp.float32
    base_decay = jnp.log(jnp.exp2(5.0 + jnp.arange(RET_HEADS, dtype=f32)) - 1.0)
    return {
        "x_prompt": jax.random.normal(ks[0], (BATCH, SEQ, D_MODEL), f32),
        "x_sample": jax.random.normal(ks[1], (DEC_BATCH, DEC_SEQ, D_MODEL), f32),
        "c_prompt": jax.random.normal(ks[2], (BATCH, D_MODEL), f32),
        "c_sample": jax.random.normal(ks[3], (DEC_BATCH, D_MODEL), f32),
        "w_ada": jax.random.normal(ks[4], (DEPTH, D_MODEL, N_MOD * D_MODEL), f32) * 0.02,
        "b_ada": jax.random.normal(ks[5], (DEPTH, N_MOD * D_MODEL), f32) * 0.01,
        "norm1_g": 1.0 + 0.02 * jax.random.normal(ks[6], (DEPTH, D_MODEL), f32),
        "w_in": jax.random.normal(ks[7], (DEPTH, D_MODEL, IN_WIDTH), f32) * D_MODEL ** -0.5,
        "attn_sink": 0.5 * jax.random.normal(ks[8], (DEPTH, ATTN_HEADS), f32),
        "ret_decay_fwd": base_decay[None] + 0.1 * jax.random.normal(ks[9], (DEPTH, RET_HEADS), f32),
        "ret_decay_bwd": base_decay[None] + 0.1 * jax.random.normal(ks[10], (DEPTH, RET_HEADS), f32),
        "w_out": jax.random.normal(ks[11], (DEPTH, MIX_WIDTH, D_MODEL), f32) * MIX_WIDTH ** -0.5,
        "norm2_g": 1.0 + 0.02 * jax.random.normal(ks[12], (DEPTH, D_MODEL), f32),
        "w_mlp1": jax.random.normal(ks[13], (DEPTH, D_MODEL, D_FF), f32) * D_MODEL ** -0.5,
        "w_mlp2": jax.random.normal(ks[14], (DEPTH, D_FF, D_MODEL), f32) * D_FF ** -0.5,
        "final_g": 1.0 + 0.02 * jax.random.normal(ks[15], (D_MODEL,), f32),
    }


def reference(x_prompt, x_sample, c_prompt, c_sample, w_ada, b_ada, norm1_g, w_in, attn_sink,
              ret_decay_fwd, ret_decay_bwd, w_out, norm2_g, w_mlp1, w_mlp2, final_g):
    y_prompt = run_trunk(x_prompt, c_prompt, w_ada, b_ada, norm1_g, w_in, attn_sink, ret_decay_fwd,
                         ret_decay_bwd, w_out, norm2_g, w_mlp1, w_mlp2, final_g)
    y_sample = run_trunk(x_sample, c_sample, w_ada, b_ada, norm1_g, w_in, attn_sink, ret_decay_fwd,
                         ret_decay_bwd, w_out, norm2_g, w_mlp1, w_mlp2, final_g)
    return (y_prompt, y_sample)
```

```python
import numpy as np
from contextlib import ExitStack

import concourse.bass as bass
import concourse.mybir as mybir
from concourse.bass_utils import run_bass_kernel_spmd

F32 = mybir.dt.float32
BF16 = mybir.dt.bfloat16
AF = mybir.ActivationFunctionType
ALU = mybir.AluOpType

D = 1024
KC = 8
C = 128
G = 512
NBLK = 27
EPS = 1e-6
LN_RS = float(np.log(128.0 ** -0.5))


class Buf:
    __slots__ = ("name", "w", "r", "excl", "al")

    def __init__(self, name, excl=False):
        self.name = name
        self.w = None
        self.r = {}
        self.excl = excl
        self.al = [self]


def alias(*bufs):
    grp = list(bufs)
    for b in bufs:
        b.al = grp


class Region:
    def __init__(self, raw, nbytes):
        self.raw = raw
        self.nbytes = nbytes
        self.items = []

    def carve(self, name, off, shape, dt):
        n = 1
        for d_ in shape[1:]:
            n *= d_
        nb = n * (4 if dt == F32 else 2)
        assert off % 4 == 0 and off + nb <= self.nbytes, (name, off, nb, self.nbytes)
        v = self.raw[:, off // 2:(off + nb) // 2]
        if dt == F32:
            v = v.bitcast(F32)
        if len(shape) == 3:
            v = v.rearrange("p (a b) -> p a b", a=shape[1])
        elif len(shape) == 4:
            v = v.rearrange("p (a b c) -> p a b c", a=shape[1], b=shape[2])
        buf = Buf(name)
        for (b2, o2, s2) in self.items:
            if off < o2 + s2 and o2 < off + nb:
                buf.al.append(b2)
                b2.al.append(buf)
        self.items.append((buf, off, nb))
        return v, buf


class Trk:
    def __init__(self, nc, es):
        self.nc = nc
        self.eng = {"pe": nc.tensor, "act": nc.scalar, "dve": nc.vector, "pool": nc.gpsimd, "sp": nc.sync}
        self.sem = {k: es.enter_context(nc.semaphore("s_" + k)) for k in ("pe", "act", "dve", "pool")}
        self.cnt = {k: 0 for k in self.sem}
        self.known = {k: {} for k in self.eng}
        self.dsem = {}
        self.es = es
        self.nwait = 0

    def dma_sem(self, name):
        self.dsem[name] = [self.es.enter_context(self.nc.semaphore("d_" + name)), 0]

    def _need(self, eng, evs):
        need = {}
        for ev in evs:
            if ev is None:
                continue
            kind, key, val = ev
            k = (kind, key)
            if need.get(k, 0) < val:
                need[k] = val
        kn = self.known[eng]
        for k, val in need.items():
            if kn.get(k, 0) >= val:
                continue
            h = self.sem[k[1]] if k[0] == "e" else self.dsem[k[1]][0]
            self.eng[eng].wait_ge(h, val)
            self.nwait += 1
            kn[k] = val

    def _collect(self, eng, reads, writes):
        evs = []
        for b in reads:
            for a in b.al:
                w = a.w
                if w is not None and not (w[0] == "e" and w[1] == eng and eng == "pe"):
                    evs.append(w)
                if a.excl:
                    for k, v in a.r.items():
                        if k != ("e", eng):
                            evs.append(v)
        for b in writes:
            for a in b.al:
                w = a.w
                if w is not None and not (w[0] == "e" and w[1] == eng and eng == "pe"):
                    evs.append(w)
                for k, v in a.r.items():
                    if not (k == ("e", eng) and eng == "pe"):
                        evs.append(v)
        return evs

    def _stamp(self, me, reads, writes):
        k = (me[0], me[1])
        for b in reads:
            old = b.r.get(k)
            if old is None or old[2] < me[2]:
                b.r[k] = me
        for b in writes:
            b.w = me
            b.r = {}

    def op(self, eng, fn, reads=(), writes=(), signal=True):
        self._need(eng, self._collect(eng, reads, writes))
        inst = fn()
        if signal:
            self.cnt[eng] += 1
            inst.then_inc(self.sem[eng], 1)
            me = ("e", eng, self.cnt[eng])
        else:
            me = ("e", eng, self.cnt[eng] + 1)
        self._stamp(me, reads, writes)
        return inst

    def dma(self, q, fn, reads, writes, semname):
        self._need(q, self._collect(q, reads, writes))
        inst = fn()
        ds = self.dsem[semname]
        ds[1] += 16
        inst.then_inc(ds[0], 16)
        me = ("d", semname, ds[1])
        self._stamp(me, reads, writes)
        return inst

    def fence(self, engines=("pe", "act", "dve", "pool", "sp")):
        evs = [("e", k, v) for k, v in self.cnt.items() if v > 0]
        evs += [("d", k, v[1]) for k, v in self.dsem.items() if v[1] > 0]
        for e in engines:
            self._need(e, [ev for ev in evs if not (ev[0] == "e" and ev[1] == e and e == "pe")])


def block_sources(l, w_in, w_out, w1, w2):
    def kin(c0, c1):
        return w_in[l][:, c0:c1].rearrange("(kc p) n -> p kc n", p=128)
    blocks = []
    blocks.append([(0, 512, kin(1280, 1792))])
    blocks.append([(0, 512, kin(1792, 2304))])
    blocks.append([(0, 64, kin(512, 576)), (64, 128, kin(512, 576)),
                   (128, 192, kin(576, 640)), (192, 256, kin(576, 640)),
                   (256, 384, kin(640, 768)), (384, 512, kin(640, 768))])
    blocks.append([(0, 256, kin(0, 256)), (256, 320, kin(512, 576)), (320, 384, kin(512, 576)),
                   (384, 512, kin(640, 768))])
    blocks.append([(0, 256, kin(256, 512)), (256, 320, kin(576, 640)), (320, 384, kin(576, 640)),
                   (384, 512, kin(640, 768))])
    for h in range(4):
        blocks.append([(0, 128, kin(768 + 128 * h, 896 + 128 * h)),
                       (128, 256, kin(1280 + 128 * h, 1408 + 128 * h)),
                       (256, 384, kin(1792 + 128 * h, 1920 + 128 * h)),
                       (384, 512, kin(2304 + 128 * h, 2432 + 128 * h))])
    for hb in range(2):
        blocks.append([("w2", 0, w_out[l][hb * 512:(hb + 1) * 512, :].rearrange("(jc p) n -> p jc n", p=128))])
    for jb in range(8):
        blocks.append([(0, 512, w1[l][:, jb * 512:(jb + 1) * 512].rearrange("(kc p) n -> p kc n", p=128))])
        blocks.append([("w2", 0, w2[l][jb * 512:(jb + 1) * 512, :].rearrange("(jc p) n -> p jc n", p=128))])
    assert len(blocks) == NBLK
    return blocks


def p3_items(NG):
    items = [("norm", 0)]
    for g in range(NG):
        for jb in range(8):
            items.append(("w1", g, jb))
            if jb >= 1:
                if jb == 7 and g + 1 < NG:
                    items.append(("w2n", g, jb - 1))
                else:
                    items.append(("w2", g, jb - 1))
        if g + 1 < NG:
            items.append(("norm2", g + 1))
        items.append(("w2", g, 7))
    return items


def build(S=4096, L=4, NS=2, NSLOT=2):
    NCH = S // C
    NG = S // G
    nc = bass.Bass("TRN2", target_bir_lowering=False)

    def din(name, shape, dt=F32):
        return nc.dram_tensor(name, shape, dt, kind="ExternalInput").ap()

    x_d = din("x", [NS, S, D])
    cv_d = din("cv", [NS, D])
    wada_d = din("w_ada", [L, D, 6 * D])
    bada_d = din("b_ada", [L, 6 * D])
    n1_d = din("norm1_g", [L, D])
    win_d = din("w_in", [L, D, 2816])
    sink_d = din("attn_sink", [L, 8])
    df_d = din("ret_decay_fwd", [L, 4])
    db_d = din("ret_decay_bwd", [L, 4])
    wout_d = din("w_out", [L, D, D])
    n2_d = din("norm2_g", [L, D])
    w1_d = din("w_mlp1", [L, D, 4 * D])
    w2_d = din("w_mlp2", [L, 4 * D, D])
    fg_d = din("final_g", [D])
    identf_d = din("identf", [128, 128])
    cpos_d = din("cpos", [128, 4 * 128])
    emask_d = din("emask", [128, 8 * 3 * 128])
    kpos_d = din("kpos", [128, 2])
    y_d = nc.dram_tensor("y", [NS, S, D], F32, kind="ExternalOutput").ap()
    wsc = nc.dram_tensor("wsc", [L * NBLK, 128, 4096], BF16).ap()

    with ExitStack() as es:
        T = Trk(nc, es)

        def sb(name, shape, dt):
            return es.enter_context(nc.sbuf_tensor(name, shape, dt))

        xT = sb("xT", [128, KC, S], F32)
        Rb_store = sb("Rb_store", [128, max(NG - 1, 1), 512], BF16)
        halo_kT = sb("halo_kT", [128, NG, 2, 128], BF16)
        halo_V = sb("halo_V", [128, NG, 2, 66], BF16)
        ring = [sb("ring%d" % i, [128, 4096], BF16) for i in range(NSLOT)]
        emask = sb("emask_sb", [128, 8, 3, 128], BF16)
        identb = sb("identb", [128, 128], BF16)
        identf = sb("identf_sb", [128, 128], F32)
        onesb = sb("onesb", [128, 128], BF16)
        cpos = sb("cpos_sb", [128, 4, 128], F32)
        kpos = sb("kpos_sb", [128, 2], F32)
        modv = sb("modv", [128, L, 48, NS], F32)
        featT = sb("featT", [128, 128], F32)
        lg = sb("lg", [128, 2, L * 4], F32)
        gCt = sb("gCt", [128, 2, L * 4], F32)
        kdec = sb("kdec", [128, 2, L * 4], F32)
        esink = sb("esink", [128, L * 8], F32)
        gm = sb("gm", [128, 2, KC], F32)
        epsb = sb("epsb", [128, 1], F32)

        b_x = [Buf("xT%d" % g) for g in range(NG)]
        b_Rbs = [Buf("Rbs%d" % g) for g in range(NG)]
        b_hk = [Buf("hk%d" % g) for g in range(NG)]
        b_hv = [Buf("hv%d" % g) for g in range(NG)]
        b_ring = [Buf("ring%d" % i) for i in range(NSLOT)]
        b_const = Buf("const")
        b_modv = Buf("modv")
        b_featT = Buf("featT")
        b_dec = Buf("dec")
        b_gm = Buf("gm")

        NF = 6
        psf = [es.enter_context(nc.psum_tensor("psf%d" % i, [128, 512], F32)) for i in range(NF)]
        psb = [es.enter_context(nc.psum_tensor("psb%d" % i, [128, 1024], BF16)) for i in range(2)]
        b_psf = [Buf("psf%d" % i, excl=True) for i in range(NF)]
        b_psb = [Buf("psb%d" % i, excl=True) for i in range(2)]
        rr = {"f": 0, "b": 0}

        def PF():
            i = rr["f"]
            rr["f"] = (i + 1) % NF
            return psf[i], b_psf[i]

        def PB():
            i = rr["b"]
            rr["b"] = (i + 1) % 2
            return psb[i], b_psb[i]

        for i in range(NSLOT):
            T.dma_sem("ring%d" % i)
        T.dma_sem("const")
        for l_ in range(L):
            T.dma_sem("cast%d" % l_)
        T.dma_sem("xin0")
        T.dma_sem("xin1")
        T.dma_sem("yout0")
        T.dma_sem("yout1")

        def mm(out, lhsT, rhs, start, stop, reads, writes, signal=None, sgc=False):
            signal = True
            return T.op("pe", lambda: nc.tensor.matmul(out, lhsT=lhsT, rhs=rhs, start=start, stop=stop, skip_group_check=sgc),
                        reads=reads, writes=writes, signal=signal)

        def tr(out, in_, ident, reads, writes):
            return T.op("pe", lambda: nc.tensor.transpose(out, in_, ident), reads=reads, writes=writes)

        def act(out, in_, func, reads, writes, scale=None, bias=None):
            kw = {}
            if scale is not None:
                kw["scale"] = scale
            if bias is not None:
                kw["bias"] = bias
            return T.op("act", lambda: nc.scalar.activation(out=out, in_=in_, func=func, **kw), reads=reads, writes=writes)

        def tt(eng, out, in0, in1, op, reads, writes):
            e = nc.vector if eng == "dve" else nc.gpsimd
            return T.op(eng, lambda: e.tensor_tensor(out=out, in0=in0, in1=in1, op=op), reads=reads, writes=writes)

        def ts(eng, out, in0, s1, s2, op0, op1, reads, writes):
            e = nc.vector if eng == "dve" else nc.gpsimd
            if op1 is None:
                return T.op(eng, lambda: e.tensor_scalar(out=out, in0=in0, scalar1=s1, scalar2=None, op0=op0), reads=reads, writes=writes)
            return T.op(eng, lambda: e.tensor_scalar(out=out, in0=in0, scalar1=s1, scalar2=s2, op0=op0, op1=op1), reads=reads, writes=writes)

        def stt(out, in0, scalar, in1, op0, op1, reads, writes):
            return T.op("dve", lambda: nc.vector.scalar_tensor_tensor(out=out, in0=in0, scalar=scalar, in1=in1, op0=op0, op1=op1),
                        reads=reads, writes=writes)

        def cp(eng, out, in_, reads, writes):
            if eng == "act":
                return act(out, in_, AF.Copy, reads, writes)
            e = nc.vector if eng == "dve" else nc.gpsimd
            return T.op(eng, lambda: e.tensor_copy(out=out, in_=in_), reads=reads, writes=writes)

        plan = []
        for l in range(L):
            for b in range(24):
                plan.append(("ada", l, b))
        for s in range(NS):
            for l in range(L):
                for g in range(NG - 1, -1, -1):
                    plan.append(("w", l, 0))
                    plan.append(("w", l, 1))
                    if g >= 1:
                        plan.append(("w", l, 2))
                for g in range(NG):
                    for b in (3, 4, 5, 6, 7, 8, 9, 10):
                        plan.append(("w", l, b))
                for it_ in p3_items(NG):
                    if it_[0] == "w1":
                        plan.append(("w", l, 11 + 2 * it_[2]))
                    elif it_[0] in ("w2", "w2n"):
                        plan.append(("w", l, 12 + 2 * it_[2]))
        b_cast = [Buf("cast%d" % l_) for l_ in range(L)]

        def cast_list(l_):
            out = []
            blocks = block_sources(l_, win_d, wout_d, w1_d, w2_d)
            for b, pieces in enumerate(blocks):
                for pc in pieces:
                    if pc[0] == "w2":
                        dst = wsc[l_ * NBLK + b].rearrange("p (jc n) -> p jc n", jc=4)
                    else:
                        dst = wsc[l_ * NBLK + b].rearrange("p (kc n) -> p kc n", kc=8)[:, :, pc[0]:pc[1]]
                    out.append((dst, pc[2]))
            return out

        def emit_casts(l_, items, last):
            for dst, src in items:
                T.dma("pool", lambda dst=dst, src=src: nc.gpsimd.dma_start(out=dst, in_=src), reads=[], writes=[],
                      semname="cast%d" % l_)
            if last:
                b_cast[l_].w = ("d", "cast%d" % l_, T.dsem["cast%d" % l_][1])
        ring_state = {"issued": 0, "cur": 0}

        def issue_load(i):
            kind, l, b = plan[i]
            slot = i % NSLOT
            if kind == "ada":
                src = wada_d[l][:, b * 256:(b + 1) * 256].rearrange("(kc p) n -> p kc n", p=128)
                dst = ring[slot][:].bitcast(F32).rearrange("p (kc n) -> p kc n", kc=8)
                T.dma("sp", lambda: nc.sync.dma_start(out=dst, in_=src), reads=[], writes=[b_ring[slot]], semname="ring%d" % slot)
            else:
                src = wsc[l * NBLK + b]
                assert b_cast[l].w is not None, ("cast not emitted yet", l, b)
                T.dma("sp", lambda: nc.sync.dma_start(out=ring[slot][:], in_=src), reads=[b_cast[l]], writes=[b_ring[slot]],
                      semname="ring%d" % slot)

        def ring_get(kind, l, b):
            i = ring_state["cur"]
            assert plan[i] == (kind, l, b), (plan[i], kind, l, b)
            while ring_state["issued"] < min(len(plan), i + NSLOT):
                issue_load(ring_state["issued"])
                ring_state["issued"] += 1
            slot = i % NSLOT
            return ring[slot], b_ring[slot]

        def ring_done():
            ring_state["cur"] += 1
            i = ring_state["cur"]
            while ring_state["issued"] < min(len(plan), i + NSLOT):
                issue_load(ring_state["issued"])
                ring_state["issued"] += 1

        with ExitStack() as es0:
            def sb0(name, shape, dt):
                return es0.enter_context(nc.sbuf_tensor(name, shape, dt))
            em32 = sb0("em32", [128, 8 * 3 * 128], F32)
            rows = sb0("rows", [128, 128], F32)
            brow = sb0("brow", [48, L, 128], F32)
            bT = sb0("bT", [128, L, 48], F32)
            draw = sb0("draw", [128, 2, L * 4], F32)
            sraw = sb0("sraw", [128, L * 8], F32)
            cact = sb0("cact", [128, KC, NS], F32)
            b_tmp0 = Buf("tmp0")
            b_rows = Buf("rows")
            b_brow = Buf("brow")
            b_bT = Buf("bT")
            b_cact = Buf("cact")

            T.op("dve", lambda: nc.vector.memset(rows[:], 0.0), writes=[b_rows])
            T.fence(engines=("sp",))
            cl = []
            cl.append((identf[:], identf_d[:, :]))
            cl.append((cpos[:].rearrange("p a b -> p (a b)"), cpos_d[:, :]))
            cl.append((kpos[:], kpos_d[:, :]))
            cl.append((em32[:], emask_d[:, :]))
            cl.append((draw[:, 0, :], df_d.rearrange("l h -> (l h)").partition_broadcast(128)))
            cl.append((draw[:, 1, :], db_d.rearrange("l h -> (l h)").partition_broadcast(128)))
            cl.append((sraw[:], sink_d.rearrange("l h -> (l h)").partition_broadcast(128)))
            cl.append((rows[0:NS * 8, :], cv_d.rearrange("s (kc p) -> (s kc) p", p=128)))
            cl.append((rows[16:16 + L * 8, :], n1_d.rearrange("l (kc p) -> (l kc) p", p=128)))
            cl.append((rows[48:48 + L * 8, :], n2_d.rearrange("l (kc p) -> (l kc) p", p=128)))
            cl.append((rows[80:88, :], fg_d.rearrange("(kc p) -> kc p", p=128)))
            for (o, i_) in cl:
                T.dma("sp", lambda o=o, i_=i_: nc.sync.dma_start(out=o, in_=i_), reads=[], writes=[], semname="const")
            for l in range(L):
                T.dma("sp", lambda l=l: nc.sync.dma_start(out=brow[:, l, :], in_=bada_d[l].rearrange("(t p) -> t p", p=128)),
                      reads=[], writes=[], semname="const")
            b_tmp0.w = ("d", "const", T.dsem["const"][1])

            emit_casts(0, cast_list(0), True)

            T.op("dve", lambda: nc.vector.memset(onesb[:], 1.0), writes=[b_const])
            T.op("dve", lambda: nc.vector.memset(epsb[:], EPS), writes=[b_const])
            cp("dve", identb[:], identf[:], [b_tmp0], [b_const])
            cp("dve", emask[:].rearrange("p a b c -> p (a b c)"), em32[:], [b_tmp0], [b_const])
            p_, bp_ = PF()
            tr(p_[:, 0:128], rows[:, :], identf[:], [b_tmp0, b_rows, b_const], [bp_])
            cp("dve", featT[:], p_[:, 0:128], [bp_], [b_featT])
            p_, bp_ = PF()
            for l in range(L):
                tr(p_[:, l * 48:(l + 1) * 48], brow[:, l, :], identf[0:48, 0:48], [b_tmp0, b_const], [bp_])
            cp("dve", bT[:].rearrange("p l t -> p (l t)"), p_[:, 0:L * 48], [bp_], [b_bT])
            for s in range(NS):
                act(cact[:, :, s], featT[:, s * 8:(s + 1) * 8], AF.Silu, [b_featT], [b_cact])
            dflat = draw[:].rearrange("p a b -> p (a b)")
            lgflat = lg[:].rearrange("p a b -> p (a b)")
            act(lgflat, dflat, AF.Exp, [b_tmp0], [b_dec], scale=-1.0)
            act(lgflat, lgflat, AF.Ln, [b_dec], [b_dec], bias=1.0)
            ts("dve", lgflat, lgflat, -1.0, None, ALU.mult, None, [b_dec], [b_dec])
            act(gCt[:].rearrange("p a b -> p (a b)"), lgflat, AF.Exp, [b_dec], [b_dec], scale=float(C))
            for d_ in range(2):
                ts("dve", kdec[:, d_, :], lg[:, d_, :], kpos[:, d_:d_ + 1], None, ALU.mult, None, [b_dec, b_tmp0], [b_dec])
            kflat = kdec[:].rearrange("p a b -> p (a b)")
            act(kflat, kflat, AF.Exp, [b_dec], [b_dec])
            ts("dve", kflat, kflat, float(128.0 ** -0.5), None, ALU.mult, None, [b_dec], [b_dec])
            act(esink[:], sraw[:], AF.Exp, [b_tmp0], [b_dec])

            for l in range(L):
                pm, bpm = PF()
                for b in range(24):
                    rt, brt = ring_get("ada", l, b)
                    w32 = rt[:].bitcast(F32).rearrange("p (kc n) -> p kc n", kc=8)
                    for j in range(2):
                        t = b * 2 + j
                        for kc in range(KC):
                            mm(pm[:, t * NS:(t + 1) * NS], w32[:, kc, j * 128:(j + 1) * 128], cact[:, kc, :],
                               kc == 0, kc == KC - 1, [brt, b_cact], [bpm])
                    ring_done()
                tt("dve", modv[:, l, :, :], pm[:, 0:48 * NS].rearrange("p (t s) -> p t s", s=NS),
                   bT[:, l, :].unsqueeze(2).broadcast_to([128, 48, NS]), ALU.add, [bpm, b_bT], [b_modv])
            T.fence()

        def prep_gm(l, s):
            for which, (goff, scoff) in enumerate(((16 + l * 8, 8), (48 + l * 8, 32))):
                stt(gm[:, which, :], modv[:, l, scoff:scoff + 8, s], 1.0, featT[:, goff:goff + 8], ALU.add, ALU.mult,
                    [b_modv, b_featT], [b_gm])

        def np1_begin():
            p_, b_ = PB()
            return p_[:].bitcast(F32), b_

        def np1_piece(g, scr, kc, pmb):
            rstd_b, tmpn, sq, b_rstd, b_tmpn, b_sq = scr
            pm, bpm = pmb
            t0 = g * G
            act(sq[kc % 2][:], xT[:, kc, t0:t0 + G], AF.Square, [b_x[g]], [b_sq[kc % 2]])
            mm(pm, onesb[:], sq[kc % 2][:], kc == 0, kc == KC - 1, [b_sq[kc % 2], b_const], [bpm])

        def np1_finish(scr, pmb):
            rstd_b, tmpn, sq, b_rstd, b_tmpn, b_sq = scr
            pm, bpm = pmb
            act(rstd_b[:], pm, AF.Ln, [bpm], [b_rstd], scale=1.0 / D, bias=epsb[:, 0:1])
            act(rstd_b[:], rstd_b[:], AF.Exp, [b_rstd], [b_rstd], scale=-0.5)

        def norm_part1(g, scr):
            pmb = np1_begin()
            for kc in range(KC):
                np1_piece(g, scr, kc, pmb)
            np1_finish(scr, pmb)

        def norm_part2(g, hT, b_hT, scr, gm_ap, sh_ap, extra_reads, kcs=None):
            rstd_b, tmpn, sq, b_rstd, b_tmpn, b_sq = scr
            t0 = g * G
            for kc in (range(KC) if kcs is None else kcs):
                stt(tmpn[kc % 2][:], xT[:, kc, t0:t0 + G], gm_ap[:, kc:kc + 1], rstd_b[:], ALU.mult, ALU.mult,
                    [b_x[g], b_rstd] + extra_reads, [b_tmpn[kc % 2]])
                act(hT[:, kc, :], tmpn[kc % 2][:], AF.Identity, [b_tmpn[kc % 2], b_modv], [b_hT[kc]], bias=sh_ap[:, kc:kc + 1])

        def norm_group(g, hT, b_hT, scr, gm_ap, sh_ap, extra_reads):
            norm_part1(g, scr)
            norm_part2(g, hT, b_hT, scr, gm_ap, sh_ap, extra_reads)

        for s in range(NS):
            with ExitStack() as es1:
                stg = [es1.enter_context(nc.sbuf_tensor("xstg%d_%d" % (s, i), [128, D], F32)) for i in range(2)]
                b_stg = [Buf("xstg%d" % i) for i in range(2)]
                for ch in range(NCH):
                    i = ch % 2
                    T.dma("sp", lambda i=i, ch=ch: nc.sync.dma_start(out=stg[i][:], in_=x_d[s, ch * C:(ch + 1) * C, :]),
                          reads=[], writes=[b_stg[i]], semname="xin%d" % i)
                    for half in range(2):
                        p_, bp_ = PF()
                        for j in range(4):
                            kc = half * 4 + j
                            tr(p_[:, j * 128:(j + 1) * 128], stg[i][:, kc * 128:(kc + 1) * 128], identf[:], [b_stg[i], b_const], [bp_])
                        cp("act" if half == 0 else "dve", xT[:, half * 4:half * 4 + 4, ch * C:(ch + 1) * C],
                           p_[:, :].rearrange("p (a b) -> p a b", a=4), [bp_], [b_x[ch // 4]])
                T.fence()

            for l in range(L):
                prep_gm(l, s)
                sh1 = modv[:, l, 0:8, s]
                g1 = modv[:, l, 16:24, s]
                sh2 = modv[:, l, 24:32, s]
                g2 = modv[:, l, 40:48, s]

                with ExitStack() as esp:
                    def sbp(name, shape, dt):
                        return esp.enter_context(nc.sbuf_tensor("%s_%d_%d" % (name, s, l), shape, dt))
                    hTs = [sbp("p1hT%d" % i, [128, KC, G], BF16) for i in range(2)]
                    b_hTs = [[Buf("hT%d_%d" % (i, k)) for k in range(KC)] for i in range(2)]
                    rstd_b = sbp("p1rstd", [128, G], F32)
                    tmpn = [sbp("p1tmpn%d" % i, [128, G], F32) for i in range(2)]
                    sq = [sbp("p1sq%d" % i, [128, G], BF16) for i in range(2)]
                    scr = (rstd_b, tmpn, sq, Buf("rstd"), [Buf("tmpn0"), Buf("tmpn1")], [Buf("sq0"), Buf("sq1")])
                    Kdb = [sbp("p1Kdb%d" % i, [128, 512], BF16) for i in range(4)]
                    Vt = [sbp("p1V%d" % i, [128, 512], BF16) for i in range(2)]
                    b_Kdb = [Buf("Kdb%d" % i) for i in range(4)]
                    b_Vt = [Buf("Vt0"), Buf("Vt1")]
                    Rb32 = sbp("p1Rb32", [128, 512], F32)
                    b_Rb32 = Buf("Rb32")
                    T.op("dve", lambda: nc.vector.memset(Rb32[:], 0.0), writes=[b_Rb32])
                    norm_group(NG - 1, hTs[(NG - 1) % 2], b_hTs[(NG - 1) % 2], scr, gm[:, 0, :], sh1, [b_gm])
                    for g in range(NG - 1, -1, -1):
                        hT, b_hT = hTs[g % 2], b_hTs[g % 2]
                        if g < NG - 1:
                            cp("act", Rb_store[:, g, :], Rb32[:], [b_Rb32], [b_Rbs[g]])
                        wk, bwk = ring_get("w", l, 0)
                        wk3 = wk[:].rearrange("p (kc n) -> p kc n", kc=8)
                        pend = []
                        pmb1 = np1_begin() if g >= 1 else None
                        for c in range(3, -1, -1):
                            pk, bpk = PF()
                            for kc in range(KC):
                                mm(pk[:, :], hT[:, kc, c * C:(c + 1) * C], wk3[:, kc, :], kc == 0, kc == KC - 1, [b_hT[kc], bwk], [bpk])
                            tt("dve", Kdb[c][:].rearrange("p (h e) -> p h e", h=4), pk[:, :].rearrange("p (h e) -> p h e", h=4),
                               kdec[:, 1, l * 4:l * 4 + 4].unsqueeze(2).broadcast_to([128, 4, 128]), ALU.mult,
                               [bpk, b_dec], [b_Kdb[c]])
                            pend.append(c)
                            if pmb1 is not None:
                                np1_piece(g - 1, scr, 2 * (3 - c), pmb1)
                                np1_piece(g - 1, scr, 2 * (3 - c) + 1, pmb1)
                        if pmb1 is not None:
                            np1_finish(scr, pmb1)
                        ring_done()
                        wv, bwv = ring_get("w", l, 1)
                        wv3 = wv[:].rearrange("p (kc n) -> p kc n", kc=8)

                        def vproj(idx):
                            c = pend[idx]
                            i = idx % 2
                            pv, bpv = PF()
                            for kc in range(KC):
                                mm(pv[:, :], hT[:, kc, c * C:(c + 1) * C], wv3[:, kc, :], kc == 0, kc == KC - 1, [b_hT[kc], bwv], [bpv])
                            cp("act", Vt[i][:], pv[:, :], [bpv], [b_Vt[i]])

                        def kvupd(idx):
                            c = pend[idx]
                            i = idx % 2
                            pkv, bpkv = PF()
                            for h in range(4):
                                mm(pkv[:, h * 128:(h + 1) * 128], Kdb[c][:, h * 128:(h + 1) * 128], Vt[i][:, h * 128:(h + 1) * 128],
                                   True, True, [b_Kdb[c], b_Vt[i]], [bpkv])
                            for h in range(4):
                                stt(Rb32[:, h * 128:(h + 1) * 128], Rb32[:, h * 128:(h + 1) * 128], gCt[:, 1, l * 4 + h:l * 4 + h + 1],
                                    pkv[:, h * 128:(h + 1) * 128], ALU.mult, ALU.add, [b_Rb32, bpkv, b_dec], [b_Rb32])

                        for idx in range(4):
                            vproj(idx)
                            if idx >= 1:
                                kvupd(idx - 1)
                            if g >= 1:
                                norm_part2(g - 1, hTs[(g - 1) % 2], b_hTs[(g - 1) % 2], scr, gm[:, 0, :], sh1, [b_gm],
                                           kcs=(2 * idx, 2 * idx + 1))
                        ring_done()
                        if g >= 1:
                            wh, bwh = ring_get("w", l, 2)
                            wh3 = wh[:].rearrange("p (kc n) -> p kc n", kc=8)
                            for kv in range(2):
                                ph, bph = PF()
                                for kc in range(KC):
                                    mm(ph[:, 0:128], wh3[:, kc, kv * 128:(kv + 1) * 128], hT[:, kc, 0:C], kc == 0, kc == KC - 1,
                                       [b_hT[kc], bwh], [bph])
                                cp("act", halo_kT[:, g, kv, :], ph[:, 0:128], [bph], [b_hk[g]])
                            ph, bph = PF()
                            for kc in range(KC):
                                mm(ph[:, 0:128], hT[:, kc, 0:C], wh3[:, kc, 256:384], kc == 0, kc == KC - 1, [b_hT[kc], bwh], [bph])
                            cp("act", halo_V[:, g, :, 0:64], ph[:, 0:128].rearrange("p (k d) -> p k d", k=2), [bph], [b_hv[g]])
                            T.op("dve", lambda g=g: nc.vector.memset(halo_V[:, g, :, 64:65], 1.0), writes=[b_hv[g]])
                            ring_done()
                        kvupd(3)
                    T.fence()

                with ExitStack() as esp:
                    def sbp(name, shape, dt):
                        return esp.enter_context(nc.sbuf_tensor("%s_%d_%d" % (name, s, l), shape, dt))
                    hT = sbp("p2hT", [128, KC, G], BF16)
                    b_hT = [Buf("hT%d" % k) for k in range(KC)]
                    mixT = sbp("p2mixT", [128, KC, G], BF16)
                    b_mixT = [Buf("mixT%d" % k) for k in range(KC)]
                    qd = sbp("p2qd", [128, 2, 4, 128], BF16)
                    DpT = sbp("p2DpT", [128, 4, 128], BF16)
                    b_lc = Buf("layerconst")
                    Rf32 = sbp("p2Rf32", [128, 512], F32)
                    b_Rf32 = Buf("Rf32")
                    kTprev = sbp("p2kTprev", [128, 2, 128], BF16)
                    Vprev = sbp("p2Vprev", [128, 2, 66], BF16)
                    b_kTprev = [Buf("kTprev0"), Buf("kTprev1")]
                    b_Vprev = Buf("Vprev")
                    den = sbp("p2den", [128, 2, 4], F32)
                    b_den = Buf("den")
                    rft = sbp("p2rft", [128, 128], F32)[:]
                    rbt = sbp("p2rbt", [128, 128], F32)[:]
                    b_rft, b_rbt = Buf("rft"), Buf("rbt")
                    sgs = [sbp("p2sgs%d" % i, [128, 128], F32)[:] for i in range(2)]
                    b_sgs = [Buf("sgs0"), Buf("sgs1")]
                    st6 = sbp("p2st6", [128, 4, 6], F32)
                    mv = sbp("p2mv", [128, 4, 2], F32)
                    rs4 = sbp("p2rs4", [128, 4], F32)
                    nm4 = sbp("p2nm4", [128, 4], F32)
                    b_stat = Buf("stat")
                    RG = 14848
                    rgraw = sbp("p2rg", [128, RG // 2], BF16)
                    R = Region(rgraw[:], RG)
                    Kdf, b_Kdf, Kdb, b_Kdb, Vr, b_Vr, RfB, b_RfB, RbB, b_RbB = [], [], [], [], [], [], [], [], [], []
                    for c in range(4):
                        v, b = R.carve("Kdf%d" % c, 0 + c * 256, [128, 128], BF16); Kdf.append(v); b_Kdf.append(b)
                        v, b = R.carve("Kdb%d" % c, 1024 + c * 256, [128, 128], BF16); Kdb.append(v); b_Kdb.append(b)
                        v, b = R.carve("Vr%d" % c, 2048 + c * 256, [128, 128], BF16); Vr.append(v); b_Vr.append(b)
                        v, b = R.carve("RfB%d" % c, 3072 + c * 256, [128, 128], BF16); RfB.append(v); b_RfB.append(b)
                        v, b = R.carve("RbB%d" % c, 7168 + c * 256, [128, 128], BF16); RbB.append(v); b_RbB.append(b)
                    QTf, b_QTf = R.carve("QTf", 4096, [128, G], BF16)
                    QTb, b_QTb = R.carve("QTb", 5120, [128, G], BF16)
                    KTr, b_KTr = R.carve("KTr", 6144, [128, G], BF16)
                    sg, b_sg = [None, None], [None, None]
                    sg[0], b_sg[0] = R.carve("sg0", 8192, [128, 4, 128], BF16)
                    sg[1], b_sg[1] = R.carve("sg1", 9216, [128, 4, 128], BF16)
                    yn, b_yn = R.carve("yn", 10240, [128, 4, 128], F32)
                    mtr, b_mtr = R.carve("mtr", 12288, [128, 4, 128], BF16)
                    AT, b_AT = [], []
                    for c in range(4):
                        v, b = R.carve("AT%d" % c, 13312 + c * 256, [128, 128], BF16); AT.append(v); b_AT.append(b)
                    Rb32, b_Rb32 = R.carve("Rb32", 14336, [128, 128], F32)
                    rstd_b, b_rstd = R.carve("rstd", 0, [128, G], F32)
                    tmpn, b_tmpn, sq, b_sq = [None, None], [None, None], [None, None], [None, None]
                    tmpn[0], b_tmpn[0] = R.carve("tmpn0", 2048, [128, G], F32)
                    tmpn[1], b_tmpn[1] = R.carve("tmpn1", 4096, [128, G], F32)
                    sq[0], b_sq[0] = R.carve("sq0", 6144, [128, G], BF16)
                    sq[1], b_sq[1] = R.carve("sq1", 7168, [128, G], BF16)
                    scr = (rstd_b, tmpn, sq, b_rstd, b_tmpn, b_sq)
                    NPT = 4
                    PT, b_PT = [], []
                    for i in range(NPT):
                        v, b = R.carve("PT%d" % i, i * 768, [128, 3, 128], BF16); PT.append(v); b_PT.append(b)
                    mta, b_mta = [None, None], [None, None]
                    mta[0], b_mta[0] = R.carve("mta0", 3072, [128, 4, 64], BF16)
                    mta[1], b_mta[1] = R.carve("mta1", 3584, [128, 4, 64], BF16)
                    QTa, b_QTa = [None, None], [None, None]
                    QTa[0], b_QTa[0] = R.carve("QTa0", 8192, [128, G], BF16)
                    QTa[1], b_QTa[1] = R.carve("QTa1", 9216, [128, G], BF16)
                    kTcur, b_kTcur = R.carve("kTcur", 10240, [128, G], BF16)
                    Vcur, b_Vcur = R.carve("Vcur", 11264, [128, 4, 2, 66], BF16)
                    dtmp, b_dtmp = R.carve("dtmp", 14336, [128, 128], F32)

                    T.op("dve", lambda: nc.vector.memset(Rf32[:], 0.0), writes=[b_Rf32])

                    nch = 4

                    def wout_half(hb, g, np1_g=None):
                        t0 = g * G
                        wo, bwo = ring_get("w", l, 9 + hb)
                        wo3 = wo[:].rearrange("p (jc n) -> p jc n", jc=4)
                        pmb = np1_begin() if np1_g is not None else None
                        for fc in range(KC):
                            pp, bpp = PF()
                            for k4 in range(4):
                                mm(pp[:, :], wo3[:, k4, fc * 128:(fc + 1) * 128], mixT[:, 4 * hb + k4, :], k4 == 0, k4 == 3,
                                   [b_mixT[4 * hb + k4], bwo], [bpp])
                            stt(xT[:, fc, t0:t0 + G], pp[:, :], g1[:, fc:fc + 1], xT[:, fc, t0:t0 + G], ALU.mult, ALU.add,
                                [bpp, b_modv, b_x[g]], [b_x[g]])
                            if pmb is not None:
                                np1_piece(np1_g, scr, fc, pmb)
                        if pmb is not None:
                            np1_finish(scr, pmb)
                        ring_done()

                    def attention(g, k):
                        wa, bwa = ring_get("w", l, 3 + k)
                        wa3 = wa[:].rearrange("p (kc n) -> p kc n", kc=8)
                        pq, bpq = PF()
                        for kc in range(KC):
                            mm(pq[:, :], wa3[:, kc, 256:384], hT[:, kc, :], kc == 0, kc == KC - 1, [b_hT[kc], bwa], [bpq])
                        cp("dve", kTcur, pq[:, :], [bpq], [b_kTcur])
                        for m in range(2):
                            pq, bpq = PF()
                            for kc in range(KC):
                                mm(pq[:, :], wa3[:, kc, m * 128:(m + 1) * 128], hT[:, kc, :], kc == 0, kc == KC - 1, [b_hT[kc], bwa], [bpq])
                            cp("act", QTa[m], pq[:, :], [bpq], [b_QTa[m]])
                        if k == 0:
                            for c in range(nch):
                                pv, bpv = PF()
                                for kc in range(KC):
                                    mm(pv[:, 0:128], hT[:, kc, c * C:(c + 1) * C], wa3[:, kc, 384:512], kc == 0, kc == KC - 1,
                                       [b_hT[kc], bwa], [bpv])
                                cp("act" if c % 2 == 0 else "dve", Vcur[:, c, :, 0:64], pv[:, 0:128].rearrange("p (k d) -> p k d", k=2), [bpv], [b_Vcur])
                            T.op("dve", lambda: nc.vector.memset(Vcur[:, :, :, 64:65], 1.0), writes=[b_Vcur])
                        ring_done()

                        def ksrc(c, b, part):
                            cc = c - 1 + b
                            if cc < 0:
                                return kTprev[64 * part:64 * part + 64, k, :], b_kTprev[k]
                            if cc >= nch:
                                return halo_kT[64 * part:64 * part + 64, g + 1, k, :], b_hk[g + 1]
                            return kTcur[64 * part:64 * part + 64, cc * C:(cc + 1) * C], b_kTcur

                        def vsrc(c, b):
                            cc = c - 1 + b
                            if cc < 0:
                                return Vprev[:, k, 0:65], b_Vprev
                            if cc >= nch:
                                return halo_V[:, g + 1, k, 0:65], b_hv[g + 1]
                            return Vcur[:, cc, k, 0:65], b_Vcur

                        units = [(c, hh) for c in range(nch) for hh in range(4)]
                        DEPTH = 3

                        def blks_of(c):
                            ch = g * 4 + c
                            return [b for b in range(3) if 0 <= ch - 1 + b < NCH]

                        def stageAB(i):
                            c, hh = units[i]
                            m, part = hh // 2, hh % 2
                            head = 4 * k + hh
                            blks = blks_of(c)
                            ps_, bps_ = psf[i % 4], b_psf[i % 4]
                            for b in blks:
                                ka, kb = ksrc(c, b, part)
                                mm(ps_[:, b * 128:(b + 1) * 128], ka, QTa[m][64 * part:64 * part + 64, c * C:(c + 1) * C],
                                   True, True, [kb, b_QTa[m]], [bps_])
                            pi = i % NPT
                            b0, b1 = blks[0], blks[-1] + 1
                            act(PT[pi][:, b0:b1, :], ps_[:, b0 * 128:b1 * 128].rearrange("p (b i) -> p b i", i=128), AF.Exp,
                                [bps_], [b_PT[pi]], scale=0.125)
                            tt("dve", PT[pi][:, b0:b1, :], PT[pi][:, b0:b1, :], emask[:, head, b0:b1, :], ALU.mult,
                               [b_PT[pi], b_const], [b_PT[pi]])

                        def stageC(i):
                            c, hh = units[i]
                            blks = blks_of(c)
                            pi = i % NPT
                            po, bpo = psf[4 + c % 2], b_psf[4 + c % 2]
                            for b in blks:
                                va, vb = vsrc(c, b)
                                mm(po[:, hh * 65:hh * 65 + 65], PT[pi][:, b, :], va, b == blks[0], b == blks[-1],
                                   [b_PT[pi], vb], [bpo])
                            if hh == 3:
                                flush_tr()
                                di = c % 2
                                po3 = po[:, 0:260].rearrange("p (h e) -> p h e", e=65)
                                tt("dve", den[:, di, :], po3[:, :, 64], esink[:, l * 8 + 4 * k:l * 8 + 4 * k + 4], ALU.add,
                                   [bpo, b_dec], [b_den])
                                T.op("dve", lambda di=di: nc.vector.reciprocal(out=den[:, di, :], in_=den[:, di, :]), reads=[b_den], writes=[b_den])
                                tt("dve", mta[di], po3[:, :, 0:64], den[:, di, :].unsqueeze(2).broadcast_to([128, 4, 64]), ALU.mult,
                                   [bpo, b_den], [b_mta[di]])
                                tr_pending.append(c)

                        tr_pending = []

                        def flush_tr():
                            while tr_pending:
                                c = tr_pending.pop(0)
                                di = c % 2
                                pb_, bpb_ = PB()
                                mflat = mta[di].rearrange("p h e -> p (h e)")
                                for m in range(2):
                                    tr(pb_[:, m * 128:(m + 1) * 128], mflat[:, m * 128:(m + 1) * 128], identb[:], [b_mta[di], b_const], [bpb_])
                                cp("act", mixT[:, 2 * k:2 * k + 2, c * C:(c + 1) * C], pb_[:, 0:256].rearrange("p (m t) -> p m t", m=2),
                                   [bpb_], [b_mixT[2 * k], b_mixT[2 * k + 1]])

                        for i in range(len(units)):
                            stageAB(i)
                            if i >= DEPTH:
                                stageC(i - DEPTH)
                        for i in range(len(units) - DEPTH, len(units)):
                            stageC(i)
                        flush_tr()
                        cp("pool", kTprev[:, k, :], kTcur[:, (nch - 1) * C:nch * C], [b_kTcur], [b_kTprev[k]])
                        if k == 1:
                            cp("pool", Vprev[:, :, 0:65], Vcur[:, nch - 1, :, 0:65], [b_Vcur], [b_Vprev])

                    def ret_A(g, h):
                        lh = l * 4 + h
                        sgh, bsgh = sg[h % 2], b_sg[h % 2]
                        wr, bwr = ring_get("w", l, 5 + h)
                        wr3 = wr[:].rearrange("p (kc n) -> p kc n", kc=8)
                        pq, bpq = PF()
                        for kc in range(KC):
                            mm(pq[:, :], wr3[:, kc, 0:128], hT[:, kc, :], kc == 0, kc == KC - 1, [b_hT[kc], bwr], [bpq])
                        pq3 = pq[:, :].rearrange("p (c i) -> p c i", c=4)
                        tt("dve", QTf.rearrange("p (c i) -> p c i", c=4), pq3, qd[:, 0, h, :].unsqueeze(1).broadcast_to([128, 4, 128]),
                           ALU.mult, [bpq, b_lc], [b_QTf])
                        tt("dve", QTb.rearrange("p (c i) -> p c i", c=4), pq3, qd[:, 1, h, :].unsqueeze(1).broadcast_to([128, 4, 128]),
                           ALU.mult, [bpq, b_lc], [b_QTb])
                        pk, bpk = PF()
                        for kc in range(KC):
                            mm(pk[:, :], wr3[:, kc, 128:256], hT[:, kc, :], kc == 0, kc == KC - 1, [b_hT[kc], bwr], [bpk])
                        cp("act", KTr, pk[:, :], [bpk], [b_KTr])
                        pts = []

                        def sig_chain(c):
                            pt_, bpt_ = pts[c]
                            sc_, bsc_ = sgs[c % 2], b_sgs[c % 2]
                            act(sc_, pt_[:, 256:384], AF.Exp, [bpt_], [bsc_], scale=-1.0)
                            act(sc_, sc_, AF.Ln, [bsc_], [bsc_], bias=1.0)
                            act(sc_, sc_, AF.Exp, [bsc_], [bsc_], scale=-1.0)
                            tt("dve", sgh[:, c, :], pt_[:, 256:384], sc_, ALU.mult, [bpt_, bsc_], [bsgh])
                        for c in range(nch):
                            pt_, bpt_ = PF()
                            pts.append((pt_, bpt_))
                            for kc in range(KC):
                                mm(pt_[:, 0:384], hT[:, kc, c * C:(c + 1) * C], wr3[:, kc, 128:512], kc == 0, kc == KC - 1,
                                   [b_hT[kc], bwr], [bpt_])
                            cp("act", Vr[c], pt_[:, 128:256], [bpt_], [b_Vr[c]])
                            ts("dve", Kdf[c], pt_[:, 0:128], kdec[:, 0, lh:lh + 1], None, ALU.mult, None, [bpt_, b_dec], [b_Kdf[c]])
                            ts("dve", Kdb[c], pt_[:, 0:128], kdec[:, 1, lh:lh + 1], None, ALU.mult, None, [bpt_, b_dec], [b_Kdb[c]])
                            if c >= 1:
                                sig_chain(c - 1)
                        sig_chain(nch - 1)
                        ring_done()

                    def ret_C(g, h):
                        lh = l * 4 + h
                        sgh, bsgh = sg[h % 2], b_sg[h % 2]
                        pkf, bpkf = PF()
                        for c in range(nch):
                            mm(pkf[:, c * 128:(c + 1) * 128], Kdf[c], Vr[c], True, True, [b_Kdf[c], b_Vr[c]], [bpkf])
                        pkb, bpkb = PF()
                        for c in range(1, nch):
                            mm(pkb[:, c * 128:(c + 1) * 128], Kdb[c], Vr[c], True, True, [b_Kdb[c], b_Vr[c]], [bpkb])
                        rfh = Rf32[:, h * 128:(h + 1) * 128]
                        cp("act", RfB[0], rfh, [b_Rf32], [b_RfB[0]])
                        if g < NG - 1:
                            cp("pool", RbB[nch - 1], Rb_store[:, g, h * 128:(h + 1) * 128], [b_Rbs[g]], [b_RbB[nch - 1]])
                            cp("dve", Rb32, Rb_store[:, g, h * 128:(h + 1) * 128], [b_Rbs[g]], [b_Rb32])
                        else:
                            T.op("pool", lambda: nc.gpsimd.memset(RbB[nch - 1], 0.0), writes=[b_RbB[nch - 1]])
                            T.op("dve", lambda: nc.vector.memset(Rb32, 0.0), writes=[b_Rb32])
                        py, bpy = PF()
                        pss = [PF() for _ in range(3)]
                        pss.append(pss[0])

                        def sT(c):
                            ps_, bps_ = pss[c]
                            mm(ps_[:, 0:128], KTr[:, c * C:(c + 1) * C], QTf[:, c * C:(c + 1) * C], True, True, [b_KTr, b_QTf], [bps_])
                        for c in range(3):
                            sT(c)
                        for c in range(nch):
                            ps_, bps_ = pss[c]
                            tt("dve", AT[c], ps_[:, 0:128], DpT[:, h, :], ALU.mult, [bps_, b_lc], [b_AT[c]])
                            if c == 0:
                                sT(3)
                            mm(py[:, c * 128:(c + 1) * 128], AT[c], Vr[c], c == 0, False, [b_AT[c], b_Vr[c]], [bpy], sgc=True)
                        fcur, bfcur, fnxt, bfnxt = rfh, b_Rf32, rft, b_rft
                        for c in range(nch):
                            stt(fnxt, fcur, gCt[:, 0, lh:lh + 1], pkf[:, c * 128:(c + 1) * 128], ALU.mult, ALU.add,
                                [bfcur, bpkf, b_dec], [bfnxt])
                            if c < nch - 1:
                                cp("act", RfB[c + 1], fnxt, [bfnxt], [b_RfB[c + 1]])
                            fcur, bfcur, fnxt, bfnxt = fnxt, bfnxt, fcur, bfcur
                        assert fcur is rfh
                        bcur, bbcur, bnxt, bbnxt = Rb32, b_Rb32, rbt, b_rbt
                        for c in range(nch - 1, 0, -1):
                            stt(bnxt, bcur, gCt[:, 1, lh:lh + 1], pkb[:, c * 128:(c + 1) * 128], ALU.mult, ALU.add,
                                [bbcur, bpkb, b_dec], [bbnxt])
                            cp("act", RbB[c - 1], bnxt, [bbnxt], [b_RbB[c - 1]])
                            bcur, bbcur, bnxt, bbnxt = bnxt, bbnxt, bcur, bbcur
                        for c in range(nch):
                            mm(py[:, c * 128:(c + 1) * 128], QTf[:, c * C:(c + 1) * C], RfB[c], False, False, [b_QTf, b_RfB[c]], [bpy], sgc=True)
                        for c in range(nch - 1, -1, -1):
                            mm(py[:, c * 128:(c + 1) * 128], QTb[:, c * C:(c + 1) * C], RbB[c], False, c == 0, [b_QTb, b_RbB[c]], [bpy], sgc=True)
                        for c in range(nch):
                            T.op("dve", lambda c=c: nc.vector.bn_stats(out=st6[:, c, :], in_=py[:, c * 128:(c + 1) * 128]),
                                 reads=[bpy], writes=[b_stat])
                        for c in range(nch):
                            T.op("dve", lambda c=c: nc.vector.bn_aggr(out=mv[:, c, :], in_=st6[:, c, :]), reads=[b_stat], writes=[b_stat])
                        act(rs4[:], mv[:, :, 1], AF.Ln, [b_stat, b_const], [b_stat], bias=epsb[:, 0:1])
                        act(rs4[:], rs4[:], AF.Exp, [b_stat], [b_stat], scale=-0.5)
                        stt(nm4[:], mv[:, :, 0], -1.0, rs4[:], ALU.mult, ALU.mult, [b_stat], [b_stat])
                        for c in range(nch):
                            ts("dve", yn[:, c, :], py[:, c * 128:(c + 1) * 128], rs4[:, c:c + 1], nm4[:, c:c + 1], ALU.mult, ALU.add,
                               [bpy, b_stat], [b_yn])
                        tt("pool", mtr, yn, sgh, ALU.mult, [b_yn, bsgh], [b_mtr])

                    def ret_B(g, h):
                        pb_, bpb_ = PB()
                        for c in range(nch):
                            tr(pb_[:, c * 128:(c + 1) * 128], mtr[:, c, :], identb[:], [b_mtr, b_const], [bpb_])
                        cp("act", mixT[:, 4 + h, :], pb_[:, 0:512], [bpb_], [b_mixT[4 + h]])

                    norm_group(0, hT, b_hT, scr, gm[:, 0, :], sh1, [b_gm])
                    for h in range(4):
                        lh = l * 4 + h
                        act(qd[:, 0, h, :], cpos[:, 2, :], AF.Exp, [b_const, b_dec], [b_lc], scale=lg[:, 0, lh:lh + 1])
                        act(qd[:, 1, h, :], cpos[:, 3, :], AF.Exp, [b_const, b_dec], [b_lc], scale=lg[:, 1, lh:lh + 1])
                        ts("dve", dtmp, cpos[:, 0, :], lg[:, 0, lh:lh + 1], None, ALU.mult, None, [b_const, b_dec], [b_dtmp])
                        stt(dtmp, cpos[:, 1, :], lg[:, 1, lh:lh + 1], dtmp, ALU.mult, ALU.add, [b_const, b_dec, b_dtmp], [b_dtmp])
                        act(dtmp, dtmp, AF.Exp, [b_dtmp], [b_dtmp])
                        ts("dve", DpT[:, h, :], dtmp, float(128.0 ** -0.5), None, ALU.mult, None, [b_dtmp], [b_lc])
                    for g in range(NG):
                        attention(g, 0)
                        attention(g, 1)
                        for h in range(4):
                            ret_A(g, h)
                            if h >= 1:
                                ret_B(g, h - 1)
                            ret_C(g, h)
                        wout_half(0, g, np1_g=(g + 1 if g + 1 < NG else None))
                        ret_B(g, 3)
                        if g + 1 < NG:
                            norm_part2(g + 1, hT, b_hT, scr, gm[:, 0, :], sh1, [b_gm])
                        wout_half(1, g)
                        if s == 0 and l + 1 < L:
                            cl_ = cast_list(l + 1)
                            per = (len(cl_) + NG - 1) // NG
                            emit_casts(l + 1, cl_[g * per:(g + 1) * per], g == NG - 1)
                    T.fence()

                with ExitStack() as esp:
                    def sbp(name, shape, dt):
                        return esp.enter_context(nc.sbuf_tensor("%s_%d_%d" % (name, s, l), shape, dt))
                    hT = sbp("p3hT", [128, KC, G], BF16)
                    b_hT = [Buf("hT%d" % k) for k in range(KC)]
                    rstd_b = sbp("p3rstd", [128, G], F32)
                    tmpn = [sbp("p3tmpn%d" % i, [128, G], F32) for i in range(2)]
                    sq = [sbp("p3sq%d" % i, [128, G], BF16) for i in range(2)]
                    scr = (rstd_b, tmpn, sq, Buf("rstd"), [Buf("tmpn0"), Buf("tmpn1")], [Buf("sq0"), Buf("sq1")])
                    hid = [sbp("p3hid%d" % i, [128, 4, G], BF16) for i in range(2)]
                    b_hid = [[Buf("hid%d_%d" % (i, j)) for j in range(4)] for i in range(2)]
                    rl = [sbp("p3rl%d" % i, [128, G], F32) for i in range(2)]
                    b_rl = [Buf("rl0"), Buf("rl1")]
                    ri = 0
                    for it_ in p3_items(NG):
                        if it_[0] == "norm":
                            norm_group(it_[1], hT, b_hT, scr, gm[:, 1, :], sh2, [b_gm])
                        elif it_[0] == "norm2":
                            norm_part2(it_[1], hT, b_hT, scr, gm[:, 1, :], sh2, [b_gm])
                        elif it_[0] == "w1":
                            g, jb = it_[1], it_[2]
                            hi = (g * 8 + jb) % 2
                            w1b, bw1 = ring_get("w", l, 11 + 2 * jb)
                            w13 = w1b[:].rearrange("p (kc n) -> p kc n", kc=8)
                            for jc in range(4):
                                ph, bph = PF()
                                for kc in range(KC):
                                    mm(ph[:, :], w13[:, kc, jc * 128:(jc + 1) * 128], hT[:, kc, :], kc == 0, kc == KC - 1, [b_hT[kc], bw1], [bph])
                                r_ = ri % 2
                                ri += 1
                                act(rl[r_][:], ph[:, :], AF.Relu, [bph], [b_rl[r_]])
                                tt("pool", hid[hi][:, jc, :], rl[r_][:], rl[r_][:], ALU.mult, [b_rl[r_]], [b_hid[hi][jc]])
                            ring_done()
                        else:
                            g, jb = it_[1], it_[2]
                            t0 = g * G
                            hi = (g * 8 + jb) % 2
                            w2b, bw2 = ring_get("w", l, 12 + 2 * jb)
                            w23 = w2b[:].rearrange("p (jc n) -> p jc n", jc=4)
                            pmb = np1_begin() if it_[0] == "w2n" else None
                            for fc in range(KC):
                                pp, bpp = PF()
                                for jc in range(4):
                                    mm(pp[:, :], w23[:, jc, fc * 128:(fc + 1) * 128], hid[hi][:, jc, :], jc == 0, jc == 3, [b_hid[hi][jc], bw2], [bpp])
                                stt(xT[:, fc, t0:t0 + G], pp[:, :], g2[:, fc:fc + 1], xT[:, fc, t0:t0 + G], ALU.mult, ALU.add,
                                    [bpp, b_modv, b_x[g]], [b_x[g]])
                                if pmb is not None:
                                    np1_piece(g + 1, scr, fc, pmb)
                            if pmb is not None:
                                np1_finish(scr, pmb)
                            ring_done()
                    T.fence()

            with ExitStack() as esp:
                def sbp(name, shape, dt):
                    return esp.enter_context(nc.sbuf_tensor("%s_%d" % (name, s), shape, dt))
                yT = sbp("fyT", [128, KC, G], F32)
                b_yT = [Buf("yT%d" % k) for k in range(KC)]
                rstd_b = sbp("frstd", [128, G], F32)
                sq = [sbp("fsq%d" % i, [128, G], BF16) for i in range(2)]
                b_sq = [Buf("sq0"), Buf("sq1")]
                b_rstd = Buf("rstd")
                ost = [sbp("fost%d" % i, [128, D], F32) for i in range(2)]
                b_ost = [Buf("ost0"), Buf("ost1")]
                oi = 0
                for g in range(NG):
                    t0 = g * G
                    pm, bpm = PF()
                    for kc in range(KC):
                        act(sq[kc % 2][:], xT[:, kc, t0:t0 + G], AF.Square, [b_x[g]], [b_sq[kc % 2]])
                        mm(pm[:, :], onesb[:], sq[kc % 2][:], kc == 0, kc == KC - 1, [b_sq[kc % 2], b_const], [bpm])
                    act(rstd_b[:], pm[:, :], AF.Ln, [bpm, b_const], [b_rstd], scale=1.0 / D, bias=epsb[:, 0:1])
                    act(rstd_b[:], rstd_b[:], AF.Exp, [b_rstd], [b_rstd], scale=-0.5)
                    for kc in range(KC):
                        stt(yT[:, kc, :], xT[:, kc, t0:t0 + G], featT[:, 80 + kc:81 + kc], rstd_b[:], ALU.mult, ALU.mult,
                            [b_x[g], b_rstd, b_featT], [b_yT[kc]])
                    for c in range(4):
                        o = oi % 2
                        oi += 1
                        for half in range(2):
                            p_, bp_ = PF()
                            for j in range(4):
                                kc = half * 4 + j
                                tr(p_[:, j * 128:(j + 1) * 128], yT[:, kc, c * C:(c + 1) * C], identf[:], [b_yT[kc], b_const], [bp_])
                            cp("act" if half == 0 else "dve", ost[o][:, half * 512:(half + 1) * 512], p_[:, :], [bp_], [b_ost[o]])
                        T.dma("sp", lambda o=o, c=c, t0=t0: nc.sync.dma_start(out=y_d[s, t0 + c * C:t0 + (c + 1) * C, :], in_=ost[o][:]),
                              reads=[b_ost[o]], writes=[], semname="yout%d" % o)
                T.fence()
        assert ring_state["cur"] == len(plan), (ring_state, len(plan))
    return nc


def host_consts():
    p = np.arange(128, dtype=np.float32)
    j = p[:, None]
    i = p[None, :]
    pf = np.maximum(i - j, 0.0) - (i + 1.0)
    pb = np.maximum(j - i, 0.0)
    iota1 = np.broadcast_to(i + 1.0, (128, 128))
    iotaC = np.broadcast_to(128.0 - i, (128, 128))
    cpos = np.stack([pf, pb, iota1, iotaC], axis=1).astype(np.float32).reshape(128, 512)
    kpos = np.stack([127.0 - p, p], axis=1).astype(np.float32)
    slopes = 2.0 ** (-8.0 * (np.arange(8, dtype=np.float64) + 1.0) / 8.0)
    em = np.zeros((128, 8, 3, 128), dtype=np.float64)
    for b in range(3):
        dist = np.abs((j + (b - 1) * 128) - i)
        valid = dist <= 128
        for h in range(8):
            em[:, h, b, :] = np.where(valid, np.exp(-slopes[h] * dist), 0.0)
    emask = em.reshape(128, 8 * 3 * 128).astype(np.float32)
    identf = np.eye(128, dtype=np.float32)
    return {"identf": identf, "cpos": np.ascontiguousarray(cpos), "emask": emask, "kpos": kpos}


_CACHE = {}


def kernel(x_prompt, x_sample, c_prompt, c_sample, w_ada, b_ada, norm1_g, w_in, attn_sink,
           ret_decay_fwd, ret_decay_bwd, w_out, norm2_g, w_mlp1, w_mlp2, final_g):
    f = lambda a: np.ascontiguousarray(np.asarray(a, dtype=np.float32))
    x_prompt, x_sample, c_prompt, c_sample = f(x_prompt), f(x_sample), f(c_prompt), f(c_sample)
    S = x_prompt.shape[1]
    L = w_ada.shape[0]
    key = (S, L)
    if key not in _CACHE:
        _CACHE[key] = build(S=S, L=L, NS=2)
    nc = _CACHE[key]
    shared = {"w_ada": f(w_ada), "b_ada": f(b_ada), "norm1_g": f(norm1_g), "w_in": f(w_in), "attn_sink": f(attn_sink),
              "ret_decay_fwd": f(ret_decay_fwd), "ret_decay_bwd": f(ret_decay_bwd), "w_out": f(w_out), "norm2_g": f(norm2_g),
              "w_mlp1": f(w_mlp1), "w_mlp2": f(w_mlp2), "final_g": f(final_g)}
    shared.update(host_consts())
    full_cores = [0, 1, 4, 5]
    half_cores = [2, 3, 6, 7]
    in_maps = [None] * 8
    for i, c in enumerate(full_cores):
        m = dict(shared)
        m["x"] = np.ascontiguousarray(x_prompt[2 * i:2 * i + 2])
        m["cv"] = np.ascontiguousarray(c_prompt[2 * i:2 * i + 2])
        in_maps[c] = m
    for i, c in enumerate(half_cores):
        m = dict(shared)
        xs = np.zeros((2, S, D), dtype=np.float32)
        xs[0] = x_sample[i]
        cs = np.zeros((2, D), dtype=np.float32)
        cs[0] = c_sample[i]
        m["x"] = xs
        m["cv"] = cs
        in_maps[c] = m
    res = run_bass_kernel_spmd(nc, in_maps, core_ids=list(range(8)))
    y_prompt = np.concatenate([res.results[c]["y"] for c in full_cores], axis=0)
    y_sample = np.stack([res.results[c]["y"][0] for c in half_cores], axis=0)
    return (y_prompt.astype(np.float32), y_sample.astype(np.float32))
```
